# Optimizing a Trainium2 kernel written in Bass

```python
import math
import jax, jax.numpy as jnp
from jax import lax
import numpy as np

D_MODEL = 1024
BATCH = 8
SEQ = 4096
DEPTH = 1

HEAD_DIM = 64
ATT_WIDTH = D_MODEL
CONV_WIDTH = D_MODEL
MIX_WIDTH = ATT_WIDTH + CONV_WIDTH
N_Q_HEADS = ATT_WIDTH // HEAD_DIM
N_KV_HEADS = 4
Q_PER_KV = N_Q_HEADS // N_KV_HEADS
KV_WIDTH = N_KV_HEADS * HEAD_DIM
N_CONV_GROUPS = CONV_WIDTH // HEAD_DIM
CONV_K = 31
DILATED_PATTERNS = ((128, 1), (512, 4), (2048, 16))
BLK = 128
NORM_EPS = 1e-6
LN_EPS = 1e-5
SPLIT_SIZES = (ATT_WIDTH, KV_WIDTH, KV_WIDTH, ATT_WIDTH, CONV_WIDTH, CONV_WIDTH, CONV_WIDTH)
IN_COLS = sum(SPLIT_SIZES)

kernel_name = "hybrid_dilated_attn_conformer_conv"


def rmsnorm(x, g):
    xf = x.astype(jnp.float32)
    y = xf * lax.rsqrt(jnp.mean(xf * xf, axis=-1, keepdims=True) + NORM_EPS)
    return (y * g.astype(jnp.float32)).astype(x.dtype)


def layernorm(x, g, b):
    xf = x.astype(jnp.float32)
    mu = jnp.mean(xf, axis=-1, keepdims=True)
    var = jnp.mean(jnp.square(xf - mu), axis=-1, keepdims=True)
    y = (xf - mu) * lax.rsqrt(var + LN_EPS)
    return (y * g.astype(jnp.float32) + b.astype(jnp.float32)).astype(x.dtype)


def alibi_slopes(n):
    return jnp.exp2(-8.0 * (jnp.arange(n, dtype=jnp.float32) + 1.0) / n)


def _to_blocks(t, dilation, n_blocks):
    b, s = t.shape[:2]
    rest = t.shape[2:]
    sub_len = s // dilation
    t = t.reshape((b, sub_len, dilation) + rest)
    t = jnp.moveaxis(t, 2, 1)
    t = jnp.pad(t, [(0, 0), (0, 0), (0, n_blocks * BLK - sub_len)] + [(0, 0)] * len(rest))
    return t.reshape((b, dilation, n_blocks, BLK) + rest)


def _from_blocks(t, seq):
    b, d, nb = t.shape[:3]
    rest = t.shape[4:]
    sub_len = seq // d
    t = t.reshape((b, d, nb * BLK) + rest)[:, :, :sub_len]
    t = jnp.moveaxis(t, 1, 2)
    return t.reshape((b, seq) + rest)


def _with_prev_block(t):
    prev = jnp.pad(t, [(0, 0), (0, 0), (1, 0)] + [(0, 0)] * (t.ndim - 3))[:, :, :-1]
    return jnp.concatenate([prev, t], axis=3)


def dilated_window_attention(q, k, v, slopes, window, dilation):
    seq = q.shape[1]
    sub_len = seq // dilation
    w = window // dilation
    nb = -(-sub_len // BLK)
    qb = _to_blocks(q, dilation, nb)
    kw = _with_prev_block(_to_blocks(k, dilation, nb))
    vw = _with_prev_block(_to_blocks(v, dilation, nb))
    s = jnp.einsum('brnqhgc,brnkhc->brnhgqk', qb, kw).astype(jnp.float32)
    qi = jnp.arange(BLK)[:, None]
    kj = jnp.arange(2 * BLK)[None, :]
    dist = BLK + qi - kj
    kpos = (jnp.arange(nb)[:, None, None] - 1) * BLK + kj
    valid = (dist >= 0) & (dist <= w) & (kpos >= 0)
    bias = -slopes[:, :, None, None] * (dist * dilation).astype(jnp.float32)
    s = jnp.where(valid[:, None, None], s + bias, -jnp.inf)
    m = jnp.max(s, axis=-1, keepdims=True)
    p = jnp.exp(s - m)
    l = jnp.sum(p, axis=-1)
    o = jnp.einsum('brnhgqk,brnkhc->brnqhgc', p, vw.astype(jnp.float32))
    l_q = jnp.moveaxis(l, -1, 3)
    o = o / l_q[..., None]
    lse = jnp.moveaxis(m[..., 0], -1, 3) + jnp.log(l_q)
    return _from_blocks(o, seq), _from_blocks(lse, seq)


def attention_branch(q, k, v, gate):
    b, s, _ = q.shape
    q = q.reshape(b, s, N_KV_HEADS, Q_PER_KV, HEAD_DIM) * (HEAD_DIM ** -0.5)
    k = k.reshape(b, s, N_KV_HEADS, HEAD_DIM)
    v = v.reshape(b, s, N_KV_HEADS, HEAD_DIM)
    slopes = alibi_slopes(N_Q_HEADS).reshape(N_KV_HEADS, Q_PER_KV)
    outs, lses = [], []
    for window, dilation in DILATED_PATTERNS:
        o, lse = dilated_window_attention(q, k, v, slopes, window, dilation)
        outs.append(o)
        lses.append(lse)
    wts = jax.nn.softmax(jnp.stack(lses, axis=0), axis=0)
    o = jnp.sum(wts[..., None] * jnp.stack(outs, axis=0), axis=0)
    o = o.reshape(b, s, ATT_WIDTH).astype(gate.dtype)
    return o * jax.nn.silu(gate)


def conv_branch(val, glu_gate, gate, conv_w, conv_b, ln_g, ln_b):
    h = val * jax.nn.sigmoid(glu_gate)
    h = lax.conv_general_dilated(
        h, conv_w.astype(h.dtype)[:, None, :], window_strides=(1,),
        padding=[(CONV_K - 1, 0)], dimension_numbers=('NWC', 'WIO', 'NWC'),
        feature_group_count=CONV_WIDTH) + conv_b.astype(h.dtype)
    h = layernorm(h, ln_g, ln_b)
    h = jax.nn.silu(h)
    return h * jax.nn.silu(gate)


def setup_inputs(seed: int = 0) -> dict:
    key = jax.random.key(seed)
    ks = jax.random.split(key, 9)
    f32 = jnp.float32
    x = jax.random.normal(ks[0], (BATCH, SEQ, D_MODEL), f32)
    norm_g = 1.0 + 0.02 * jax.random.normal(ks[1], (DEPTH, D_MODEL), f32)
    w_in = jax.random.normal(ks[2], (DEPTH, D_MODEL, IN_COLS), f32) * D_MODEL ** -0.5
    conv_w = jax.random.normal(ks[3], (DEPTH, CONV_K, CONV_WIDTH), f32) * CONV_K ** -0.5
    conv_b = 0.02 * jax.random.normal(ks[4], (DEPTH, CONV_WIDTH), f32)
    conv_ln_g = 1.0 + 0.02 * jax.random.normal(ks[5], (DEPTH, CONV_WIDTH), f32)
    conv_ln_b = 0.02 * jax.random.normal(ks[6], (DEPTH, CONV_WIDTH), f32)
    w_out = jax.random.normal(ks[7], (DEPTH, MIX_WIDTH, D_MODEL), f32) * MIX_WIDTH ** -0.5
    final_norm_g = 1.0 + 0.02 * jax.random.normal(ks[8], (D_MODEL,), f32)
    return {"x": x, "norm_g": norm_g, "w_in": w_in, "conv_w": conv_w, "conv_b": conv_b,
            "conv_ln_g": conv_ln_g, "conv_ln_b": conv_ln_b, "w_out": w_out,
            "final_norm_g": final_norm_g}


def reference(x, norm_g, w_in, conv_w, conv_b, conv_ln_g, conv_ln_b, w_out, final_norm_g):
    split_idx = list(np.cumsum(SPLIT_SIZES)[:-1])
    for layer in range(DEPTH):
        h = rmsnorm(x, norm_g[layer])
        proj = jnp.einsum('bsd,de->bse', h, w_in[layer])
        q, k, v, a_gate, c_val, c_glu, c_gate = jnp.split(proj, split_idx, axis=-1)
        y_att = attention_branch(q, k, v, a_gate)
        y_conv = conv_branch(c_val, c_glu, c_gate, conv_w[layer], conv_b[layer],
                             conv_ln_g[layer], conv_ln_b[layer])
        y = jnp.concatenate([y_att, y_conv], axis=-1)
        x = x + jnp.einsum('bse,ed->bsd', y, w_out[layer])
    return rmsnorm(x, final_norm_g)
```

```python
import numpy as np
import concourse.bass as bass
import concourse.mybir as mybir
from concourse.bass_utils import run_bass_kernel_spmd

F32 = mybir.dt.float32
BF16 = mybir.dt.bfloat16
AF = mybir.ActivationFunctionType
ALU = mybir.AluOpType

NORM_EPS = 1e-6
LN_EPS = 1e-5
BIG = 30000.0
SLOPES = [float(2.0 ** (-8.0 * (h + 1) / 16.0)) for h in range(16)]
NTAB = 9


class _Op:
    __slots__ = ("eng", "fn", "deps", "marked", "stream", "ndma", "dma_val", "ticket", "seq")


class Prog:
    ENGS = ("pe", "act", "dve", "pool", "sp")

    def __init__(self):
        self.ops = {e: [] for e in self.ENGS}
        self.lastw = {}
        self.readers = {}
        self.streams = {}
        self.nseq = 0

    @staticmethod
    def _ek(op):
        return ("s", op.stream) if op.stream is not None else ("e", op.eng)

    def add(self, eng, fn, reads=(), writes=(), stream=None, ndma=1):
        op = _Op()
        op.eng = eng
        op.fn = fn
        op.marked = False
        op.stream = stream
        op.ndma = ndma
        op.ticket = None
        op.dma_val = None
        op.seq = self.nseq
        self.nseq += 1
        deps = {}

        def dep(d):
            if d is op:
                return
            k = self._ek(d)
            o = deps.get(k)
            if o is None or o.seq < d.seq:
                deps[k] = d

        for k in reads:
            w = self.lastw.get(k)
            if w is not None:
                dep(w)
        for k in writes:
            w = self.lastw.get(k)
            if w is not None:
                dep(w)
            for r in self.readers.get(k, {}).values():
                dep(r)
        if stream is not None:
            st = self.streams.setdefault(stream, [None, 0])
            if st[0] is not None:
                dep(st[0])
            st[1] += 16 * ndma
            op.dma_val = st[1]
            st[0] = op
        op.deps = []
        for d in deps.values():
            if d.eng == "pe" and eng == "pe" and d.stream is None and stream is None:
                continue
            op.deps.append(d)
            d.marked = True
        for k in writes:
            self.lastw[k] = op
            self.readers[k] = {}
        for k in reads:
            self.readers.setdefault(k, {})[self._ek(op)] = op
        self.ops[eng].append(op)
        return op

    def emit(self, nc, block, esem, ssem, final_waits):
        for e in self.ENGS:
            cnt = 0
            for op in self.ops[e]:
                if op.stream is None and op.marked:
                    cnt += 1
                    op.ticket = cnt

        def run(ename, e):
            seen = {}
            for op in self.ops[ename]:
                need = {}
                for d in op.deps:
                    if d.stream is not None:
                        key = ("s", d.stream)
                        val = d.dma_val
                    else:
                        key = ("e", d.eng)
                        val = d.ticket
                    if need.get(key, 0) < val:
                        need[key] = val
                for key, val in need.items():
                    if seen.get(key, 0) >= val:
                        continue
                    seen[key] = val
                    sem = ssem[key[1]] if key[0] == "s" else esem[key[1]]
                    e.wait_ge(sem, val)
                if op.stream is not None:
                    op.fn(e, ssem[op.stream])
                else:
                    ins = op.fn(e)
                    if op.marked:
                        ins.then_inc(esem[ename], 1)
            for st in final_waits.get(ename, ()):
                if st in self.streams:
                    e.wait_ge(ssem[st], self.streams[st][1])

        @block.tensor
        def _(e):
            run("pe", e)

        @block.scalar
        def _(e):
            run("act", e)

        @block.vector
        def _(e):
            run("dve", e)

        @block.gpsimd
        def _(e):
            run("pool", e)

        @block.sync
        def _(e):
            run("sp", e)


class Buf:
    def __init__(self, ap):
        self.a = ap
        self.t = ap.tensor
        self.F = ap.shape[1]

    def v(self, p0, npart, off, dims):
        return bass.AP(self.t, p0 * self.F + off, [[self.F, npart]] + [list(d) for d in dims])


def DA(ap, off, dims):
    return bass.AP(ap.tensor, off, [list(d) for d in dims])


def build_dtab():
    a = np.arange(128)
    d = a[None, :] - a[:, None]
    tabs = []
    for dl in range(5):
        D = 512 * dl + 4 * d
        valid = (d % 4 == 0) & (D >= 0) & (D <= 2048)
        tabs.append(np.where(valid, -D.astype(np.float64), -BIG))
    for dl in range(2):
        D = 512 * dl + 4 * d
        valid = (D >= 0) & (D <= 512)
        tabs.append(np.where(valid, -D.astype(np.float64), -BIG))
    cq = 4 * (a % 32) + a // 32
    d1 = cq[None, :] - a[:, None]
    for be in range(2):
        D = 128 * be + d1
        valid = (D >= 0) & (D <= 128)
        tabs.append(np.where(valid, -D.astype(np.float64), -BIG))
    return np.ascontiguousarray(np.concatenate(tabs, axis=1).astype(np.float32))


def build(NT):
    nc = bass.Bass("TRN2", target_bir_lowering=False)
    S = NT * 512
    x_d = nc.dram_tensor("x", [S, 1024], F32, kind="ExternalInput").ap()
    win_d = nc.dram_tensor("w_in", [1024, 5632], F32, kind="ExternalInput").ap()
    wout_d = nc.dram_tensor("w_out", [2048, 1024], F32, kind="ExternalInput").ap()
    g_d = nc.dram_tensor("g", [1, 1024], F32, kind="ExternalInput").ap()
    fg_d = nc.dram_tensor("fg", [1, 1024], F32, kind="ExternalInput").ap()
    cw_d = nc.dram_tensor("cw", [128, 248], F32, kind="ExternalInput").ap()
    vec_d = nc.dram_tensor("vec", [128, 24], F32, kind="ExternalInput").ap()
    dtab_d = nc.dram_tensor("dtab", [128, NTAB * 128], F32, kind="ExternalInput").ap()
    ident_d = nc.dram_tensor("ident", [128, 128], F32, kind="ExternalInput").ap()
    out_d = nc.dram_tensor("out", [S, 1024], F32, kind="ExternalOutput").ap()
    wsc_d = nc.dram_tensor("wsc", [23, 128, 2048], BF16).ap()
    wosc_d = nc.dram_tensor("wosc", [2, 16, 128, 512], BF16).ap()

    P = Prog()
    from contextlib import ExitStack
    with ExitStack() as es:
        def sb(name, F, dt):
            return Buf(es.enter_context(nc.sbuf_tensor(name, [128, F], dt))[:])

        def ps(name):
            return Buf(es.enter_context(nc.psum_tensor(name, [128, 512], F32))[:])

        wring = sb("wring", 3 * 2048, BF16)
        woring = sb("woring", 4 * 512, BF16)
        xt = sb("xt", 2 * 1024, F32)
        hb = sb("hb", 2 * 1024, BF16)
        hT = sb("hT", 8 * 512, BF16)
        qT = sb("qT", 8 * 512, BF16)
        Kr = sb("Kr", 5 * 2 * 512, BF16)
        NVB = 28
        Vb = sb("Vb", NVB * 256, BF16)
        ag = sb("ag", 8 * 512, BF16)
        cg = sb("cg", 8 * 512, BF16)
        hc = sb("hc", 8 * 542, BF16)
        sig = sb("sig", 2 * 512, BF16)
        Et = sb("Et", NTAB * 16 * 128, BF16)
        pT = sb("pT", 5 * 512, BF16)
        rcb = sb("rcb", 2 * 512, F32)
        yT = sb("yT", 16 * 512, BF16)
        cvo = sb("cvo", 8 * 512, BF16)
        xb16 = sb("xb16", 2 * 512, BF16)
        xsq = sb("xsq", 2 * 512, BF16)
        lnw = sb("lnw", 4 * 512, F32)
        diag = sb("diag", 16 * 128, BF16)
        res = sb("res", 4 * 1024, F32)
        gbc = sb("gbc", 1024, F32)
        fgbc = sb("fgbc", 1024, F32)
        identb = sb("identb", 128, BF16)
        onesb = sb("onesb", 128, BF16)
        cw = sb("cw_sb", 248, F32)
        vec = sb("vec_sb", 24, F32)
        stat = sb("stat", 64, F32)
        epsc = sb("epsc", 2, F32)
        mmb = [ps("mm0"), ps("mm1")]
        scb = [ps("sc0"), ps("sc1")]
        accb = [ps("acc0"), ps("acc1"), ps("acc2"), ps("acc3")]
        mmk = ["mm0", "mm1"]
        sck = ["sc0", "sc1"]
        acck = ["acc0", "acc1", "acc2", "acc3"]
        scb3 = [scb[0], scb[1], mmb[1], mmb[0]]
        sck3 = ["sc0", "sc1", "mm1", "mm0"]
        ipb = [mmb[1], accb[2], accb[3]]
        ipk = ["mm1", "acc2", "acc3"]
        dtabv = Buf(cvo.a.bitcast(F32))

        stream_names = ["c0", "c1", "c2", "c3", "x0", "x1", "w0", "w1", "w2", "wo0", "wo1", "wo2", "wo3",
                        "r0", "r1", "r2", "r3", "s0", "s1", "s2", "s3", "pa", "pb", "qa", "qb", "id"]
        esem = {e: es.enter_context(nc.semaphore("sem_" + e)) for e in ("pe", "act", "dve", "pool", "sp")}
        ssem = {s: es.enter_context(nc.semaphore("dsem_" + s)) for s in stream_names}

        YT_ALL = [("yT", c) for c in range(16)]
        CVO_ALL = [("cvo", c) for c in range(8)]

        def dma1(out, in_):
            def f(e, sem):
                e.dma_start(out=out, in_=in_).then_inc(sem, 16)
            return f

        def dman(pairs):
            def f(e, sem):
                for o, i in pairs:
                    e.dma_start(out=o, in_=i).then_inc(sem, 16)
            return f

        P.add("sp", dma1(gbc.a, DA(g_d, 0, [[0, 128], [1, 1024]])), writes=["gbc"], stream="c0")
        P.add("sp", dma1(fgbc.a, DA(fg_d, 0, [[0, 128], [1, 1024]])), writes=["fgbc"], stream="c1")
        P.add("sp", dma1(cw.a, cw_d), writes=["cw"], stream="c2")
        P.add("sp", dma1(vec.a, vec_d), writes=["vec"], stream="c3")
        P.add("sp", dma1(dtabv.v(0, 128, 0, [[1, NTAB * 128]]), dtab_d), writes=CVO_ALL, stream="c0")
        P.add("pool", dma1(identb.a, ident_d), writes=["identb"], stream="id")
        P.add("dve", lambda e: e.memset(onesb.a, 1.0), writes=["onesb"])
        P.add("dve", lambda e: e.memset(epsc.v(0, 128, 0, [[1, 1]]), NORM_EPS), writes=["eps"])
        P.add("dve", lambda e: e.memset(epsc.v(0, 128, 1, [[1, 1]]), LN_EPS), writes=["eps"])
        P.add("pool", lambda e: e.memset(hc.a, 0.0), writes=[("hc", c) for c in range(8)])
        for tab in range(NTAB):
            for h in range(16):
                g_, i_ = h // 4, h % 4
                if tab < 7:
                    segs = [((tab * 4 + g_) * 512 + b_ * 128 + i_ * 32, tab * 128 + 32 * b_, 32) for b_ in range(4)]
                else:
                    segs = [((tab * 4 + g_) * 512 + i_ * 128, tab * 128, 128)]
                for eo, io, ln in segs:
                    o = Et.v(0, 128, eo, [[1, ln]])
                    i = dtabv.v(0, 128, io, [[1, ln]])
                    P.add("act", (lambda e, o=o, i=i, h=h: e.activation(out=o, in_=i, func=AF.Exp, scale=SLOPES[h])),
                          reads=CVO_ALL, writes=[("Et", tab)])
        for dl in range(2):
            o = Et.v(0, 128, dl * 2048, [[1, 2048]])
            i = Et.v(0, 128, (5 + dl) * 2048, [[1, 2048]])
            P.add("dve", (lambda e, o=o, i=i: e.tensor_tensor(out=o, in0=o, in1=i, op=ALU.add)),
                  reads=[("Et", dl), ("Et", 5 + dl)], writes=[("Et", dl)])
        stg = [(0, [("yT", c) for c in range(0, 6)]), (3072, [("yT", c) for c in range(6, 12)])]
        si = 0
        for kc in range(8):
            for hh in range(2):
                soff, skeys = stg[si % 2]
                pst = "pa" if si % 2 == 0 else "pb"
                qst = "qa" if si % 2 == 0 else "qb"
                si += 1
                P.add("pool", dma1(yT.v(0, 128, soff, [[1, 2816]]),
                                   DA(win_d, 128 * kc * 5632 + 2816 * hh, [[5632, 128], [1, 2816]])),
                      writes=skeys, stream=pst)
                pairs = []
                if hh == 0:
                    for pr in range(2):
                        for i2 in range(2):
                            for ci in range(2):
                                src = yT.v(0, 128, soff + 64 * (8 * pr + 2 * i2 + ci), [[256, 2], [1, 64]])
                                dst = DA(wsc_d, (2 * pr + i2) * 262144 + ci * 1024 + kc * 128,
                                         [[2048, 128], [64, 2], [1, 64]])
                                pairs.append((dst, src))
                    ng, g0, c0 = 7, 4, 1024
                else:
                    ng, g0, c0 = 11, 11, 0
                for ci in range(2):
                    src = yT.v(0, 128, soff + c0 + 128 * ci, [[256, ng], [1, 128]])
                    dst = DA(wsc_d, g0 * 262144 + ci * 1024 + kc * 128, [[2048, 128], [262144, ng], [1, 128]])
                    pairs.append((dst, src))
                P.add("sp", dman(pairs), reads=skeys, writes=["wsc"], stream=qst, ndma=len(pairs))
        for k2 in range(8):
            soff, skeys = stg[si % 2]
            pst = "pa" if si % 2 == 0 else "pb"
            qst = "qa" if si % 2 == 0 else "qb"
            si += 1
            P.add("pool", dma1(yT.v(0, 128, soff, [[1024, 2], [1, 1024]]),
                               DA(wout_d, 256 * k2 * 1024, [[1024, 128], [131072, 2], [1, 1024]])),
                  writes=skeys, stream=pst)
            pairs = []
            for kk in range(2):
                src = yT.v(0, 128, soff + 1024 * kk, [[512, 2], [1, 512]])
                dst = DA(wosc_d, (2 * k2 + kk) * 65536, [[512, 128], [1048576, 2], [1, 512]])
                pairs.append((dst, src))
            P.add("sp", dman(pairs), reads=skeys, writes=["wosc"], stream=qst, ndma=2)

        cnt = {"ip": 0, "mm": 0, "sc": 0, "pt": 0, "wr": 0, "wo": 0, "ev": 0, "dg": 0, "rc": 0, "st": 0}

        def nxt(k, m):
            v = cnt[k] % m
            cnt[k] += 1
            return v

        def pre(n):
            for G in range(4):
                xs = G % 2
                P.add("sp", dma1(xt.v(0, 128, xs * 1024, [[1, 1024]]),
                                 DA(x_d, (512 * n + 128 * G) * 1024, [[1024, 128], [1, 1024]])),
                      writes=[("xt", xs)], stream="x%d" % xs)
                sc0 = nxt("st", 8) * 4
                skey = ("stat", sc0)
                xin = xt.v(0, 128, xs * 1024, [[1, 1024]])
                P.add("act", (lambda e, xin=xin, sc0=sc0: e.activation(
                    out=xsq.v(0, 128, 0, [[1, 1024]]), in_=xin, func=AF.Square, accum_out=stat.v(0, 128, sc0, [[1, 1]]))),
                    reads=[("xt", xs)], writes=[skey, ("xsq", 0), ("xsq", 1)])
                P.add("act", (lambda e, sc0=sc0: e.activation(
                    out=stat.v(0, 128, sc0 + 1, [[1, 1]]), in_=stat.v(0, 128, sc0, [[1, 1]]), func=AF.Ln,
                    scale=1.0 / 1024.0, bias=epsc.v(0, 128, 0, [[1, 1]]))),
                    reads=[skey, "eps"], writes=[skey])
                P.add("act", (lambda e, sc0=sc0: e.activation(
                    out=stat.v(0, 128, sc0 + 2, [[1, 1]]), in_=stat.v(0, 128, sc0 + 1, [[1, 1]]), func=AF.Exp,
                    scale=-0.5)),
                    reads=[skey], writes=[skey])
                hout = hb.v(0, 128, xs * 1024, [[1, 1024]])
                P.add("dve", (lambda e, xin=xin, hout=hout, sc0=sc0: e.scalar_tensor_tensor(
                    out=hout, in0=xin, scalar=stat.v(0, 128, sc0 + 2, [[1, 1]]), in1=gbc.a,
                    op0=ALU.mult, op1=ALU.mult)),
                    reads=[("xt", xs), skey, "gbc"], writes=[("hb", xs)])
                b = nxt("sc", 4)
                pv = Buf(scb3[b].a.bitcast(BF16))
                for kc in range(8):
                    P.add("pe", (lambda e, pv=pv, kc=kc, xs=xs: e.transpose(
                        out=pv.v(0, 128, 128 * kc, [[1, 128]]),
                        in_=hb.v(0, 128, xs * 1024 + 128 * kc, [[1, 128]]), identity=identb.a)),
                        reads=[("hb", xs), "identb"], writes=[sck3[b]])
                eng = "act" if nxt("ev", 2) == 0 else "dve"
                o = hT.v(0, 128, 128 * G, [[512, 8], [1, 128]])
                i = pv.v(0, 128, 0, [[128, 8], [1, 128]])
                if eng == "act":
                    P.add("act", (lambda e, o=o, i=i: e.activation(out=o, in_=i, func=AF.Copy)),
                          reads=[sck3[b]], writes=[("hT", G)])
                else:
                    P.add("dve", (lambda e, o=o, i=i: e.tensor_copy(out=o, in_=i)),
                          reads=[sck3[b]], writes=[("hT", G)])
                yield

        HT_ALL = [("hT", G) for G in range(4)]

        def vblk_nat(B):
            return B % 8

        def vblk_str(m, R):
            return 8 + (m % 5) * 4 + R

        def inproj(n):
            order = [14, 10, 15, 11, 16, 12, 17, 13, 6, 7, 8, 9, 0, 1, 2, 3, 4, 5, 18, 19, 20, 21]
            for gi in order:
                slot = nxt("wr", 3)
                wkey = ("wr", slot)
                P.add("sp", dma1(wring.v(0, 128, slot * 2048, [[1, 2048]]),
                                 DA(wsc_d, gi * 262144, [[2048, 128], [1, 2048]])),
                      reads=["wsc"], writes=[wkey], stream="w%d" % slot)
                if gi == 5:
                    blocks = []
                    for bb in range(4):
                        blocks.append((128 * bb, 1, vblk_nat(4 * n + bb)))
                    for R in range(4):
                        blocks.append((R, 4, vblk_str(n, R)))
                    for poff, pstr, blk in blocks:
                        b = nxt("ip", 3)
                        for kc in range(8):
                            lhsT = hT.v(0, 128, kc * 512 + poff, [[pstr, 128]])
                            rhs = wring.v(0, 128, slot * 2048 + kc * 128, [[1024, 2], [1, 128]])
                            o = ipb[b].v(0, 128, 0, [[1, 256]])
                            P.add("pe", (lambda e, o=o, lhsT=lhsT, rhs=rhs, kc=kc: e.matmul(
                                o, lhsT=lhsT, rhs=rhs, start=(kc == 0), stop=(kc == 7))),
                                reads=HT_ALL + [wkey], writes=[ipk[b]])
                        o = Vb.v(0, 128, blk * 256, [[1, 256]])
                        i = ipb[b].v(0, 128, 0, [[1, 256]])
                        if nxt("ev", 2) == 0:
                            P.add("act", (lambda e, o=o, i=i: e.activation(out=o, in_=i, func=AF.Copy)),
                                  reads=[ipk[b]], writes=[("Vb", blk)])
                        else:
                            P.add("dve", (lambda e, o=o, i=i: e.tensor_copy(out=o, in_=i)),
                                  reads=[ipk[b]], writes=[("Vb", blk)])
                    yield
                    continue
                for ci in range(2):
                    cid = 2 * gi + ci
                    b = nxt("ip", 3)
                    o = ipb[b].a
                    for kc in range(8):
                        lhsT = wring.v(0, 128, slot * 2048 + ci * 1024 + kc * 128, [[1, 128]])
                        rhs = hT.v(0, 128, kc * 512, [[1, 512]])
                        P.add("pe", (lambda e, o=o, lhsT=lhsT, rhs=rhs, kc=kc: e.matmul(
                            o, lhsT=lhsT, rhs=rhs, start=(kc == 0), stop=(kc == 7))),
                            reads=HT_ALL + [wkey], writes=[ipk[b]])
                    i = ipb[b].a
                    if cid < 8:
                        o2 = qT.v(0, 128, cid * 512, [[128, 4], [1, 128]])
                        i2 = ipb[b].v(0, 128, 0, [[1, 4], [4, 128]])
                        P.add("dve", (lambda e, o2=o2, i2=i2: e.tensor_scalar(
                            out=o2, in0=i2, scalar1=0.125, scalar2=None, op0=ALU.mult)),
                            reads=[ipk[b]], writes=[("qT", cid)])
                    elif cid < 10:
                        pr = cid - 8
                        o2 = Kr.v(0, 128, ((n % 5) * 2 + pr) * 512, [[1, 512]])
                        P.add("dve", (lambda e, o2=o2, i=i: e.tensor_copy(out=o2, in_=i)),
                              reads=[ipk[b]], writes=[("Kr", n % 5, pr)])
                    elif 12 <= cid < 20:
                        c = cid - 12
                        o2 = ag.v(0, 128, c * 512, [[1, 512]])
                        P.add("act", (lambda e, o2=o2, i=i: e.activation(out=o2, in_=i, func=AF.Silu)),
                              reads=[ipk[b]], writes=[("ag", c)])
                    elif 20 <= cid < 28:
                        c = cid - 20
                        o2 = hc.v(0, 128, c * 542 + 30, [[1, 512]])
                        s2 = sig.v(0, 128, (c % 2) * 512, [[1, 512]])
                        P.add("dve", (lambda e, o2=o2, i=i, s2=s2: e.tensor_tensor(
                            out=o2, in0=i, in1=s2, op=ALU.mult)),
                            reads=[ipk[b], ("sig", c % 2)], writes=[("hc", c)])
                    elif 28 <= cid < 36:
                        c = cid - 28
                        o2 = sig.v(0, 128, (c % 2) * 512, [[1, 512]])
                        P.add("act", (lambda e, o2=o2, i=i: e.activation(out=o2, in_=i, func=AF.Sigmoid)),
                              reads=[ipk[b]], writes=[("sig", c % 2)])
                    else:
                        c = cid - 36
                        o2 = cg.v(0, 128, c * 512, [[1, 512]])
                        P.add("act", (lambda e, o2=o2, i=i: e.activation(out=o2, in_=i, func=AF.Silu)),
                              reads=[ipk[b]], writes=[("cg", c)])
                yield

        def attention(n):
            units = []
            for g in range(4):
                first = True
                for bb in range(4):
                    Bq = 4 * n + bb
                    for be in range(2):
                        Bk = Bq - be
                        if Bk < 0:
                            continue
                        units.append(dict(g=g, kind="n", qidx=bb, m=Bk // 4, koff=128 * (Bk % 4), kstr=1,
                                          qoff=128 * bb, vblk=vblk_nat(Bk), tab=7 + be, fin=-1, first=first))
                        first = False
                for R in range(4):
                    ul = []
                    for dl in range(min(4, n) + 1):
                        ul.append(dict(g=g, kind="s", qidx=R, m=n - dl, koff=R, kstr=4, qoff=R,
                                       vblk=vblk_str(n - dl, R), tab=dl, fin=-1, first=False))
                    ul[-1]["fin"] = R
                    units += ul

            def front(u):
                g = u["g"]
                pr, half = g // 2, g % 2
                P0 = 64 * half
                sb_ = nxt("sc", 4)
                m = u["m"]
                lhsT = Kr.v(P0, 64, ((m % 5) * 2 + pr) * 512 + u["koff"], [[u["kstr"], 128]])
                if u["kind"] == "s":
                    rhs = qT.v(P0, 64, 4 * pr * 512 + 128 * u["qidx"], [[32, 4], [512, 4], [1, 32]])
                else:
                    rhs = qT.v(P0, 64, 4 * pr * 512 + 32 * u["qidx"], [[512, 4], [128, 4], [1, 32]])
                o = scb3[sb_].a
                P.add("pe", (lambda e, o=o, lhsT=lhsT, rhs=rhs: e.matmul(o, lhsT=lhsT, rhs=rhs, start=True, stop=True)),
                      reads=[("Kr", m % 5, pr)] + [("qT", 4 * pr + i) for i in range(4)], writes=[sck3[sb_]])
                pb = nxt("pt", 5)
                u["pb"] = pb
                po = pT.v(0, 128, pb * 512, [[1, 512]])
                P.add("act", (lambda e, po=po, o=o: e.activation(out=po, in_=o, func=AF.Exp)),
                      reads=[sck3[sb_]], writes=[("pT", pb)])
                ev = Et.v(0, 128, (u["tab"] * 4 + g) * 512, [[1, 512]])
                P.add("dve", (lambda e, po=po, ev=ev: e.tensor_tensor(out=po, in0=po, in1=ev, op=ALU.mult)),
                      reads=[("pT", pb), ("Et", u["tab"])], writes=[("pT", pb)])

            started = {}

            def back(u):
                g = u["g"]
                pb = u["pb"]
                vblk = u["vblk"]
                po = pT.v(0, 128, pb * 512, [[1, 512]])
                if u["first"]:
                    for Rb in range(4):
                        started[Rb] = [False, False]
                lv = Vb.v(0, 128, vblk * 256 + g * 64, [[1, 64]])
                lo = onesb.v(0, 128, 0, [[1, 64]])
                if u["kind"] == "s":
                    parts = [(u["qidx"], [[1, 512]], 0, po)]
                else:
                    parts = [(Rp, [[1, 128]], 128 * u["qidx"],
                              pT.v(0, 128, pb * 512 + 32 * Rp, [[128, 4], [1, 32]])) for Rp in range(4)]
                for Rb, odims, ooff, rap in parts:
                    for hv, lt in ((0, lv), (1, lo)):
                        st = not started[Rb][hv]
                        started[Rb][hv] = True
                        oap = accb[Rb].v(64 * hv, 64, ooff, odims)
                        P.add("pe", (lambda e, oap=oap, lt=lt, rap=rap, st=st, hv=hv: e.matmul(
                            oap, lhsT=lt, rhs=rap, start=st, stop=True, skip_group_check=True,
                            tile_position=(0, 64 * hv))),
                            reads=[("Vb", vblk), "onesb", ("pT", pb)], writes=[acck[Rb]])
                if u["fin"] >= 0:
                    R = u["fin"]
                    rs = nxt("rc", 2)
                    ro = rcb.v(0, 64, rs * 512, [[1, 512]])
                    P.add("act", (lambda e, ro=ro, R=R: e.activation(out=ro, in_=accb[R].v(64, 64, 0, [[1, 512]]), func=AF.Ln)),
                          reads=[acck[R]], writes=[("rcb", rs)])
                    P.add("act", (lambda e, ro=ro: e.activation(out=ro, in_=ro, func=AF.Exp, scale=-1.0)),
                          reads=[("rcb", rs)], writes=[("rcb", rs)])
                    for par in range(2):
                        o = yT.v(64 * par, 64, 2 * g * 512 + 128 * R, [[512, 2], [1, 128]])
                        i0 = accb[R].v(0, 64, 32 * par, [[64, 2], [128, 4], [1, 32]])
                        i1 = rcb.v(0, 64, rs * 512 + 32 * par, [[64, 2], [128, 4], [1, 32]])
                        P.add("dve", (lambda e, o=o, i0=i0, i1=i1: e.tensor_tensor(out=o, in0=i0, in1=i1, op=ALU.mult)),
                              reads=[acck[R], ("rcb", rs)], writes=[("yT", 2 * g), ("yT", 2 * g + 1)])
                    o = yT.v(0, 128, 2 * g * 512 + 128 * R, [[512, 2], [1, 128]])
                    a2 = ag.v(0, 128, 2 * g * 512 + R, [[512, 2], [4, 128]])
                    P.add("dve", (lambda e, o=o, a2=a2: e.tensor_tensor(out=o, in0=o, in1=a2, op=ALU.mult)),
                          reads=[("yT", 2 * g), ("yT", 2 * g + 1), ("ag", 2 * g), ("ag", 2 * g + 1)],
                          writes=[("yT", 2 * g), ("yT", 2 * g + 1)])

            AHEAD = 4
            for i in range(min(AHEAD, len(units))):
                front(units[i])
            for i, u in enumerate(units):
                back(u)
                yield
                if i + AHEAD < len(units):
                    front(units[i + AHEAD])

        def conv(n):
            for c in range(8):
                b = 0
                for k0 in range(0, 31, 8):
                    ks = list(range(k0, min(k0 + 8, 31)))
                    slots = []
                    for k in ks:
                        ds = nxt("dg", 16)
                        slots.append(ds)
                        dgo = diag.v(0, 128, ds * 128, [[1, 128]])
                        P.add("pool", (lambda e, dgo=dgo, c=c, k=k: e.tensor_scalar(
                            out=dgo, in0=identb.a, scalar1=cw.v(0, 128, c * 31 + k, [[1, 1]]), scalar2=0.0,
                            op0=ALU.mult, op1=ALU.add)),
                            reads=["identb", "cw"], writes=[("diag", ds)])
                    for k, ds in zip(ks, slots):
                        dgo = diag.v(0, 128, ds * 128, [[1, 128]])
                        rap = hc.v(0, 128, c * 542 + k, [[1, 512]])
                        rd = [("diag", d_) for d_ in slots] if k == ks[0] else [("diag", ds)]
                        P.add("pe", (lambda e, b=b, dgo=dgo, rap=rap, k=k: e.matmul(
                            mmb[b].a, lhsT=dgo, rhs=rap, start=(k == 0), stop=(k == 30))),
                            reads=rd + [("hc", c)], writes=[mmk[b]])
                    yield
                co = cvo.v(0, 128, c * 512, [[1, 512]])
                P.add("act", (lambda e, co=co, b=b, c=c: e.activation(
                    out=co, in_=mmb[b].a, func=AF.Identity, bias=vec.v(0, 128, c, [[1, 1]]))),
                    reads=[mmk[b], "vec"], writes=[("cvo", c)])
            P.add("pool", lambda e: e.tensor_copy(out=hc.v(0, 128, 0, [[542, 8], [1, 30]]),
                                                  in_=hc.v(0, 128, 512, [[542, 8], [1, 30]])),
                  reads=[("hc", c) for c in range(8)], writes=[("hc", c) for c in range(8)])
            yield

        def lnorm(n):
            for c in range(8):
                co = cvo.v(0, 128, c * 512, [[1, 512]])
                xs_ = c % 2
                qo = xsq.v(0, 128, xs_ * 512, [[1, 512]])
                if c % 2 == 0:
                    P.add("act", (lambda e, qo=qo, co=co: e.activation(out=qo, in_=co, func=AF.Square)),
                          reads=[("cvo", c)], writes=[("xsq", xs_)])
                else:
                    P.add("dve", (lambda e, qo=qo, co=co: e.tensor_tensor(out=qo, in0=co, in1=co, op=ALU.mult)),
                          reads=[("cvo", c)], writes=[("xsq", xs_)])
                for (bank, key, rap, rkey) in ((scb[0], sck[0], co, ("cvo", c)), (scb[1], sck[1], qo, ("xsq", xs_))):
                    P.add("pe", (lambda e, bank=bank, rap=rap, c=c: e.matmul(
                        bank.a, lhsT=onesb.a, rhs=rap, start=(c == 0), stop=(c == 7))),
                        reads=["onesb", rkey], writes=[key])
                yield
            mean = lnw.v(0, 128, 0, [[1, 512]])
            msq = lnw.v(0, 128, 512, [[1, 512]])
            rstd = lnw.v(0, 128, 1024, [[1, 512]])
            P.add("act", lambda e: e.activation(out=mean, in_=scb[0].a, func=AF.Copy, scale=1.0 / 1024.0),
                  reads=[sck[0]], writes=[("ln", 0)])
            P.add("act", lambda e: e.activation(out=msq, in_=scb[0].a, func=AF.Square, scale=1.0 / 1024.0),
                  reads=[sck[0]], writes=[("ln", 1)])
            P.add("dve", lambda e: e.scalar_tensor_tensor(out=msq, in0=scb[1].a, scalar=1.0 / 1024.0, in1=msq,
                                                          op0=ALU.mult, op1=ALU.subtract),
                  reads=[sck[1], ("ln", 1)], writes=[("ln", 1)])
            P.add("act", lambda e: e.activation(out=msq, in_=msq, func=AF.Sqrt, bias=epsc.v(0, 128, 1, [[1, 1]])),
                  reads=[("ln", 1), "eps"], writes=[("ln", 1)])
            P.add("dve", lambda e: e.reciprocal(out=rstd, in_=msq), reads=[("ln", 1)], writes=[("ln", 2)])
            for c in range(8):
                co = cvo.v(0, 128, c * 512, [[1, 512]])
                t = lnw.v(0, 128, 1536, [[1, 512]])
                P.add("dve", (lambda e, co=co, t=t: e.tensor_tensor(out=t, in0=co, in1=mean, op=ALU.subtract)),
                      reads=[("cvo", c), ("ln", 0)], writes=[("ln", 3)])
                P.add("dve", (lambda e, t=t: e.tensor_tensor(out=t, in0=t, in1=rstd, op=ALU.mult)),
                      reads=[("ln", 3), ("ln", 2)], writes=[("ln", 3)])
                z = xb16.v(0, 128, (c % 2) * 512, [[1, 512]])
                P.add("act", (lambda e, z=z, t=t, c=c: e.activation(
                    out=z, in_=t, func=AF.Silu, scale=vec.v(0, 128, 8 + c, [[1, 1]]), bias=vec.v(0, 128, 16 + c, [[1, 1]]))),
                    reads=[("ln", 3), "vec"], writes=[("xb16", c % 2)])
                yo = yT.v(0, 128, (8 + c) * 512, [[1, 512]])
                gi_ = cg.v(0, 128, c * 512, [[1, 512]])
                P.add("dve", (lambda e, yo=yo, z=z, gi_=gi_: e.tensor_tensor(out=yo, in0=z, in1=gi_, op=ALU.mult)),
                      reads=[("xb16", c % 2), ("cg", c)], writes=[("yT", 8 + c)])
                yield

        wo_pref = {}

        def wo_dma(hf, kc):
            ws = nxt("wo", 4)
            P.add("pool", dma1(woring.v(0, 128, ws * 512, [[1, 512]]),
                               DA(wosc_d, hf * 1048576 + kc * 65536, [[512, 128], [1, 512]])),
                  reads=["wosc"], writes=[("wo", ws)], stream="wo%d" % ws)
            return ws

        def outproj_prefetch(n):
            wo_pref[n] = [wo_dma(0, kc) for kc in range(4)]

        def outproj(n):
            for R in range(4):
                P.add("sp", dma1(res.v(0, 128, R * 1024, [[1, 1024]]),
                                 DA(x_d, (512 * n + R) * 1024, [[4 * 1024, 128], [1, 1024]])),
                      writes=[("res", R)], stream="r%d" % R)
            for hf in range(2):
                banks = (accb, acck) if hf == 0 else (scb3, sck3)
                for kc in range(16):
                    if hf == 0 and kc < 4 and wo_pref.get(n):
                        ws = wo_pref[n][kc]
                    else:
                        ws = wo_dma(hf, kc)
                    for R in range(4):
                        if kc < 8:
                            lhsT = yT.v(0, 128, kc * 512 + 128 * R, [[1, 128]])
                        else:
                            lhsT = yT.v(0, 128, kc * 512 + R, [[4, 128]])
                        rhs = woring.v(0, 128, ws * 512, [[1, 512]])
                        P.add("pe", (lambda e, R=R, lhsT=lhsT, rhs=rhs, kc=kc, bk=banks[0]: e.matmul(
                            bk[R].a, lhsT=lhsT, rhs=rhs, start=(kc == 0), stop=(kc == 15))),
                            reads=[("yT", kc), ("wo", ws)], writes=[banks[1][R]])
                for R in range(4):
                    r = res.v(0, 128, R * 1024 + 512 * hf, [[1, 512]])
                    P.add("dve", (lambda e, r=r, R=R, bk=banks[0]: e.tensor_tensor(out=r, in0=bk[R].a, in1=r, op=ALU.add)),
                          reads=[banks[1][R], ("res", R)], writes=[("res", R)])

        def outproj_tail(n):
            for R in range(4):
                r = res.v(0, 128, R * 1024, [[1, 1024]])
                sc0 = nxt("st", 8) * 4
                skey = ("stat", sc0)
                jo = xsq.v(0, 128, 0, [[1, 1024]])
                P.add("act", (lambda e, r=r, sc0=sc0, jo=jo: e.activation(
                    out=jo, in_=r, func=AF.Square, accum_out=stat.v(0, 128, sc0, [[1, 1]]))),
                    reads=[("res", R)], writes=[skey, ("xsq", 0), ("xsq", 1)])
                P.add("act", (lambda e, sc0=sc0: e.activation(
                    out=stat.v(0, 128, sc0 + 1, [[1, 1]]), in_=stat.v(0, 128, sc0, [[1, 1]]), func=AF.Ln,
                    scale=1.0 / 1024.0, bias=epsc.v(0, 128, 0, [[1, 1]]))),
                    reads=[skey, "eps"], writes=[skey])
                P.add("act", (lambda e, sc0=sc0: e.activation(
                    out=stat.v(0, 128, sc0 + 2, [[1, 1]]), in_=stat.v(0, 128, sc0 + 1, [[1, 1]]), func=AF.Exp,
                    scale=-0.5)),
                    reads=[skey], writes=[skey])
                P.add("dve", (lambda e, r=r, sc0=sc0: e.scalar_tensor_tensor(
                    out=r, in0=r, scalar=stat.v(0, 128, sc0 + 2, [[1, 1]]), in1=fgbc.a,
                    op0=ALU.mult, op1=ALU.mult)),
                    reads=[("res", R), skey, "fgbc"], writes=[("res", R)])
                P.add("pool", dma1(DA(out_d, (512 * n + R) * 1024, [[4 * 1024, 128], [1, 1024]]), r),
                      reads=[("res", R)], writes=["out"], stream="s%d" % R)
                yield

        def interleave(*gw):
            gens = [[g_, w_] for g_, w_ in gw]
            while gens:
                for it in list(gens):
                    for _ in range(it[1]):
                        try:
                            next(it[0])
                        except StopIteration:
                            gens.remove(it)
                            break

        def adv(gen, k):
            if gen is None:
                return None
            for _ in range(k):
                try:
                    next(gen)
                except StopIteration:
                    return None
            return gen

        def inproj_phase(n, ln_tile):
            gi, gc = inproj(n), conv(n)
            gl = lnorm(ln_tile) if ln_tile is not None else None
            step = 0
            while gi is not None or gc is not None or gl is not None:
                gl = adv(gl, 1)
                gi = adv(gi, 1)
                if step >= 8:
                    gc = adv(gc, 3)
                step += 1

        interleave((pre(0), 1))
        inproj_phase(0, None)
        for n in range(NT):
            tails = [(outproj_tail(n - 1), 1)] if n > 0 else []
            if n + 1 < NT:
                interleave((attention(n), 12), (pre(n + 1), 1), *tails)
                outproj_prefetch(n)
                inproj_phase(n + 1, n)
            else:
                interleave((attention(n), 12), *tails)
                outproj_prefetch(n)
                interleave((lnorm(n), 1))
            outproj(n)
        interleave((outproj_tail(NT - 1), 1))

        with nc.Block() as block:
            P.emit(nc, block, esem, ssem, {"pool": ["s0", "s1", "s2", "s3"]})
    return nc


_CACHE = {}


def _common_inputs(norm_g, w_in, conv_w, conv_b, conv_ln_g, conv_ln_b, w_out, final_norm_g):
    f = np.float32
    cw = np.ascontiguousarray(
        np.asarray(conv_w, f)[0].reshape(31, 8, 128).transpose(2, 1, 0).reshape(128, 248))
    vecs = [np.asarray(v, f)[0].reshape(8, 128).T for v in (conv_b, conv_ln_g, conv_ln_b)]
    vec = np.ascontiguousarray(np.concatenate(vecs, axis=1))
    return {
        "w_in": np.ascontiguousarray(np.asarray(w_in, f)[0]),
        "w_out": np.ascontiguousarray(np.asarray(w_out, f)[0]),
        "g": np.ascontiguousarray(np.asarray(norm_g, f).reshape(1, 1024)),
        "fg": np.ascontiguousarray(np.asarray(final_norm_g, f).reshape(1, 1024)),
        "cw": cw,
        "vec": vec,
        "dtab": build_dtab(),
        "ident": np.eye(128, dtype=f),
    }


def kernel(x, norm_g, w_in, conv_w, conv_b, conv_ln_g, conv_ln_b, w_out, final_norm_g):
    x = np.asarray(x, np.float32)
    B, S, _ = x.shape
    NT = S // 512
    if NT not in _CACHE:
        _CACHE[NT] = build(NT)
    nc = _CACHE[NT]
    common = _common_inputs(norm_g, w_in, conv_w, conv_b, conv_ln_g, conv_ln_b, w_out, final_norm_g)
    in_maps = []
    for b in range(B):
        m = dict(common)
        m["x"] = np.ascontiguousarray(x[b])
        in_maps.append(m)
    res = run_bass_kernel_spmd(nc, in_maps, core_ids=list(range(B)))
    return np.stack([np.asarray(r["out"], np.float32) for r in res.results], axis=0)
```

```python
import numpy as np
import concourse.bass as bass
import concourse.mybir as mybir
from concourse.bass_utils import run_bass_kernel_spmd

F32 = mybir.dt.float32
BF16 = mybir.dt.bfloat16
AF = mybir.ActivationFunctionType
ALU = mybir.AluOpType

NORM_EPS = 1e-6
LN_EPS = 1e-5
BIG = 30000.0
SLOPES = [float(2.0 ** (-8.0 * (h + 1) / 16.0)) for h in range(16)]
NTAB = 9


class _Op:
    __slots__ = ("eng", "fn", "deps", "marked", "stream", "ndma", "dma_val", "ticket", "seq")


class Prog:
    ENGS = ("pe", "act", "dve", "pool", "sp")

    def __init__(self):
        self.ops = {e: [] for e in self.ENGS}
        self.lastw = {}
        self.readers = {}
        self.streams = {}
        self.nseq = 0

    @staticmethod
    def _ek(op):
        return ("s", op.stream) if op.stream is not None else ("e", op.eng)

    def add(self, eng, fn, reads=(), writes=(), stream=None, ndma=1):
        op = _Op()
        op.eng = eng
        op.fn = fn
        op.marked = False
        op.stream = stream
        op.ndma = ndma
        op.ticket = None
        op.dma_val = None
        op.seq = self.nseq
        self.nseq += 1
        deps = {}

        def dep(d):
            if d is op:
                return
            k = self._ek(d)
            o = deps.get(k)
            if o is None or o.seq < d.seq:
                deps[k] = d

        for k in reads:
            w = self.lastw.get(k)
            if w is not None:
                dep(w)
        for k in writes:
            w = self.lastw.get(k)
            if w is not None:
                dep(w)
            for r in self.readers.get(k, {}).values():
                dep(r)
        if stream is not None:
            st = self.streams.setdefault(stream, [None, 0])
            if st[0] is not None:
                dep(st[0])
            st[1] += 16 * ndma
            op.dma_val = st[1]
            st[0] = op
        op.deps = []
        for d in deps.values():
            if d.eng == "pe" and eng == "pe" and d.stream is None and stream is None:
                continue
            op.deps.append(d)
            d.marked = True
        for k in writes:
            self.lastw[k] = op
            self.readers[k] = {}
        for k in reads:
            self.readers.setdefault(k, {})[self._ek(op)] = op
        self.ops[eng].append(op)
        return op

    def emit(self, nc, block, esem, ssem, final_waits):
        for e in self.ENGS:
            cnt = 0
            for op in self.ops[e]:
                if op.stream is None and op.marked:
                    cnt += 1
                    op.ticket = cnt

        def run(ename, e):
            seen = {}
            for op in self.ops[ename]:
                need = {}
                for d in op.deps:
                    if d.stream is not None:
                        key = ("s", d.stream)
                        val = d.dma_val
                    else:
                        key = ("e", d.eng)
                        val = d.ticket
                    if need.get(key, 0) < val:
                        need[key] = val
                for key, val in need.items():
                    if seen.get(key, 0) >= val:
                        continue
                    seen[key] = val
                    sem = ssem[key[1]] if key[0] == "s" else esem[key[1]]
                    e.wait_ge(sem, val)
                if op.stream is not None:
                    op.fn(e, ssem[op.stream])
                else:
                    ins = op.fn(e)
                    if op.marked:
                        ins.then_inc(esem[ename], 1)
            for st in final_waits.get(ename, ()):
                if st in self.streams:
                    e.wait_ge(ssem[st], self.streams[st][1])

        @block.tensor
        def _(e):
            run("pe", e)

        @block.scalar
        def _(e):
            run("act", e)

        @block.vector
        def _(e):
            run("dve", e)

        @block.gpsimd
        def _(e):
            run("pool", e)

        @block.sync
        def _(e):
            run("sp", e)


class Buf:
    def __init__(self, ap):
        self.a = ap
        self.t = ap.tensor
        self.F = ap.shape[1]

    def v(self, p0, npart, off, dims):
        return bass.AP(self.t, p0 * self.F + off, [[self.F, npart]] + [list(d) for d in dims])


def DA(ap, off, dims):
    return bass.AP(ap.tensor, off, [list(d) for d in dims])


def build_dtab():
    a = np.arange(128)
    d = a[None, :] - a[:, None]
    tabs = []
    for dl in range(5):
        D = 512 * dl + 4 * d
        valid = (d % 4 == 0) & (D >= 0) & (D <= 2048)
        tabs.append(np.where(valid, -D.astype(np.float64), -BIG))
    for dl in range(2):
        D = 512 * dl + 4 * d
        valid = (D >= 0) & (D <= 512)
        tabs.append(np.where(valid, -D.astype(np.float64), -BIG))
    cq = 4 * (a % 32) + a // 32
    d1 = cq[None, :] - a[:, None]
    for be in range(2):
        D = 128 * be + d1
        valid = (D >= 0) & (D <= 128)
        tabs.append(np.where(valid, -D.astype(np.float64), -BIG))
    return np.ascontiguousarray(np.concatenate(tabs, axis=1).astype(np.float32))


def build(NT):
    nc = bass.Bass("TRN2", target_bir_lowering=False)
    S = NT * 512
    x_d = nc.dram_tensor("x", [S, 1024], F32, kind="ExternalInput").ap()
    win_d = nc.dram_tensor("w_in", [1024, 5632], F32, kind="ExternalInput").ap()
    wout_d = nc.dram_tensor("w_out", [2048, 1024], F32, kind="ExternalInput").ap()
    g_d = nc.dram_tensor("g", [1, 1024], F32, kind="ExternalInput").ap()
    fg_d = nc.dram_tensor("fg", [1, 1024], F32, kind="ExternalInput").ap()
    cw_d = nc.dram_tensor("cw", [128, 248], F32, kind="ExternalInput").ap()
    vec_d = nc.dram_tensor("vec", [128, 24], F32, kind="ExternalInput").ap()
    dtab_d = nc.dram_tensor("dtab", [128, NTAB * 128], F32, kind="ExternalInput").ap()
    ident_d = nc.dram_tensor("ident", [128, 128], F32, kind="ExternalInput").ap()
    out_d = nc.dram_tensor("out", [S, 1024], F32, kind="ExternalOutput").ap()
    wsc_d = nc.dram_tensor("wsc", [23, 128, 2048], BF16).ap()
    wosc_d = nc.dram_tensor("wosc", [2, 16, 128, 512], BF16).ap()

    P = Prog()
    from contextlib import ExitStack
    with ExitStack() as es:
        def sb(name, F, dt):
            return Buf(es.enter_context(nc.sbuf_tensor(name, [128, F], dt))[:])

        def ps(name):
            return Buf(es.enter_context(nc.psum_tensor(name, [128, 512], F32))[:])

        wring = sb("wring", 3 * 2048, BF16)
        woring = sb("woring", 4 * 512, BF16)
        xt = sb("xt", 2 * 1024, F32)
        hb = sb("hb", 2 * 1024, BF16)
        hT = sb("hT", 8 * 512, BF16)
        qT = sb("qT", 8 * 512, BF16)
        Kr = sb("Kr", 5 * 2 * 512, BF16)
        NVB = 28
        Vb = sb("Vb", NVB * 256, BF16)
        ag = sb("ag", 8 * 512, BF16)
        cg = sb("cg", 8 * 512, BF16)
        hc = sb("hc", 8 * 542, BF16)
        sig = sb("sig", 2 * 512, BF16)
        Et = sb("Et", NTAB * 16 * 128, BF16)
        pT = sb("pT", 5 * 512, BF16)
        rcb = sb("rcb", 2 * 512, F32)
        yT = sb("yT", 16 * 512, BF16)
        cvo = sb("cvo", 8 * 512, BF16)
        xb16 = sb("xb16", 2 * 512, BF16)
        xsq = sb("xsq", 2 * 512, BF16)
        lnw = sb("lnw", 4 * 512, F32)
        diag = sb("diag", 16 * 128, BF16)
        res = sb("res", 4 * 1024, F32)
        gbc = sb("gbc", 1024, F32)
        fgbc = sb("fgbc", 1024, F32)
        identb = sb("identb", 128, BF16)
        onesb = sb("onesb", 128, BF16)
        cw = sb("cw_sb", 248, F32)
        vec = sb("vec_sb", 24, F32)
        stat = sb("stat", 64, F32)
        epsc = sb("epsc", 2, F32)
        mmb = [ps("mm0"), ps("mm1")]
        scb = [ps("sc0"), ps("sc1")]
        accb = [ps("acc0"), ps("acc1"), ps("acc2"), ps("acc3")]
        mmk = ["mm0", "mm1"]
        sck = ["sc0", "sc1"]
        acck = ["acc0", "acc1", "acc2", "acc3"]
        scb3 = [scb[0], scb[1], mmb[1], mmb[0]]
        sck3 = ["sc0", "sc1", "mm1", "mm0"]
        ipb = [mmb[1], accb[2], accb[3]]
        ipk = ["mm1", "acc2", "acc3"]
        dtabv = Buf(cvo.a.bitcast(F32))

        stream_names = ["c0", "c1", "c2", "c3", "x0", "x1", "w0", "w1", "w2", "wo0", "wo1", "wo2", "wo3",
                        "r0", "r1", "r2", "r3", "s0", "s1", "s2", "s3", "pa", "pb", "qa", "qb", "id"]
        esem = {e: es.enter_context(nc.semaphore("sem_" + e)) for e in ("pe", "act", "dve", "pool", "sp")}
        ssem = {s: es.enter_context(nc.semaphore("dsem_" + s)) for s in stream_names}

        YT_ALL = [("yT", c) for c in range(16)]
        CVO_ALL = [("cvo", c) for c in range(8)]

        def dma1(out, in_):
            def f(e, sem):
                e.dma_start(out=out, in_=in_).then_inc(sem, 16)
            return f

        def dman(pairs):
            def f(e, sem):
                for o, i in pairs:
                    e.dma_start(out=o, in_=i).then_inc(sem, 16)
            return f

        P.add("sp", dma1(gbc.a, DA(g_d, 0, [[0, 128], [1, 1024]])), writes=["gbc"], stream="c0")
        P.add("sp", dma1(fgbc.a, DA(fg_d, 0, [[0, 128], [1, 1024]])), writes=["fgbc"], stream="c1")
        P.add("sp", dma1(cw.a, cw_d), writes=["cw"], stream="c2")
        P.add("sp", dma1(vec.a, vec_d), writes=["vec"], stream="c3")
        P.add("sp", dma1(dtabv.v(0, 128, 0, [[1, NTAB * 128]]), dtab_d), writes=CVO_ALL, stream="c0")
        P.add("pool", dma1(identb.a, ident_d), writes=["identb"], stream="id")
        P.add("dve", lambda e: e.memset(onesb.a, 1.0), writes=["onesb"])
        P.add("dve", lambda e: e.memset(epsc.v(0, 128, 0, [[1, 1]]), NORM_EPS), writes=["eps"])
        P.add("dve", lambda e: e.memset(epsc.v(0, 128, 1, [[1, 1]]), LN_EPS), writes=["eps"])
        P.add("pool", lambda e: e.memset(hc.a, 0.0), writes=[("hc", c) for c in range(8)])
        for tab in range(NTAB):
            for h in range(16):
                g_, i_ = h // 4, h % 4
                if tab < 7:
                    segs = [((tab * 4 + g_) * 512 + b_ * 128 + i_ * 32, tab * 128 + 32 * b_, 32) for b_ in range(4)]
                else:
                    segs = [((tab * 4 + g_) * 512 + i_ * 128, tab * 128, 128)]
                for eo, io, ln in segs:
                    o = Et.v(0, 128, eo, [[1, ln]])
                    i = dtabv.v(0, 128, io, [[1, ln]])
                    P.add("act", (lambda e, o=o, i=i, h=h: e.activation(out=o, in_=i, func=AF.Exp, scale=SLOPES[h])),
                          reads=CVO_ALL, writes=[("Et", tab)])
        for dl in range(2):
            o = Et.v(0, 128, dl * 2048, [[1, 2048]])
            i = Et.v(0, 128, (5 + dl) * 2048, [[1, 2048]])
            P.add("dve", (lambda e, o=o, i=i: e.tensor_tensor(out=o, in0=o, in1=i, op=ALU.add)),
                  reads=[("Et", dl), ("Et", 5 + dl)], writes=[("Et", dl)])
        stg = [(0, [("yT", c) for c in range(0, 6)]), (3072, [("yT", c) for c in range(6, 12)])]
        si = 0
        for kc in range(8):
            for hh in range(2):
                soff, skeys = stg[si % 2]
                pst = "pa" if si % 2 == 0 else "pb"
                qst = "qa" if si % 2 == 0 else "qb"
                si += 1
                P.add("pool", dma1(yT.v(0, 128, soff, [[1, 2816]]),
                                   DA(win_d, 128 * kc * 5632 + 2816 * hh, [[5632, 128], [1, 2816]])),
                      writes=skeys, stream=pst)
                pairs = []
                if hh == 0:
                    for pr in range(2):
                        for i2 in range(2):
                            for ci in range(2):
                                src = yT.v(0, 128, soff + 64 * (8 * pr + 2 * i2 + ci), [[256, 2], [1, 64]])
                                dst = DA(wsc_d, (2 * pr + i2) * 262144 + ci * 1024 + kc * 128,
                                         [[2048, 128], [64, 2], [1, 64]])
                                pairs.append((dst, src))
                    ng, g0, c0 = 7, 4, 1024
                else:
                    ng, g0, c0 = 11, 11, 0
                for ci in range(2):
                    src = yT.v(0, 128, soff + c0 + 128 * ci, [[256, ng], [1, 128]])
                    dst = DA(wsc_d, g0 * 262144 + ci * 1024 + kc * 128, [[2048, 128], [262144, ng], [1, 128]])
                    pairs.append((dst, src))
                P.add("sp", dman(pairs), reads=skeys, writes=["wsc"], stream=qst, ndma=len(pairs))
        for k2 in range(8):
            soff, skeys = stg[si % 2]
            pst = "pa" if si % 2 == 0 else "pb"
            qst = "qa" if si % 2 == 0 else "qb"
            si += 1
            P.add("pool", dma1(yT.v(0, 128, soff, [[1024, 2], [1, 1024]]),
                               DA(wout_d, 256 * k2 * 1024, [[1024, 128], [131072, 2], [1, 1024]])),
                  writes=skeys, stream=pst)
            pairs = []
            for kk in range(2):
                src = yT.v(0, 128, soff + 1024 * kk, [[512, 2], [1, 512]])
                dst = DA(wosc_d, (2 * k2 + kk) * 65536, [[512, 128], [1048576, 2], [1, 512]])
                pairs.append((dst, src))
            P.add("sp", dman(pairs), reads=skeys, writes=["wosc"], stream=qst, ndma=2)

        cnt = {"ip": 0, "mm": 0, "sc": 0, "pt": 0, "wr": 0, "wo": 0, "ev": 0, "dg": 0, "rc": 0, "st": 0}

        def nxt(k, m):
            v = cnt[k] % m
            cnt[k] += 1
            return v

        def pre(n):
            for G in range(4):
                xs = G % 2
                P.add("sp", dma1(xt.v(0, 128, xs * 1024, [[1, 1024]]),
                                 DA(x_d, (512 * n + 128 * G) * 1024, [[1024, 128], [1, 1024]])),
                      writes=[("xt", xs)], stream="x%d" % xs)
                sc0 = nxt("st", 8) * 4
                skey = ("stat", sc0)
                xin = xt.v(0, 128, xs * 1024, [[1, 1024]])
                P.add("act", (lambda e, xin=xin, sc0=sc0: e.activation(
                    out=xsq.v(0, 128, 0, [[1, 1024]]), in_=xin, func=AF.Square, accum_out=stat.v(0, 128, sc0, [[1, 1]]))),
                    reads=[("xt", xs)], writes=[skey, ("xsq", 0), ("xsq", 1)])
                P.add("act", (lambda e, sc0=sc0: e.activation(
                    out=stat.v(0, 128, sc0 + 1, [[1, 1]]), in_=stat.v(0, 128, sc0, [[1, 1]]), func=AF.Ln,
                    scale=1.0 / 1024.0, bias=epsc.v(0, 128, 0, [[1, 1]]))),
                    reads=[skey, "eps"], writes=[skey])
                P.add("act", (lambda e, sc0=sc0: e.activation(
                    out=stat.v(0, 128, sc0 + 2, [[1, 1]]), in_=stat.v(0, 128, sc0 + 1, [[1, 1]]), func=AF.Exp,
                    scale=-0.5)),
                    reads=[skey], writes=[skey])
                hout = hb.v(0, 128, xs * 1024, [[1, 1024]])
                P.add("dve", (lambda e, xin=xin, hout=hout, sc0=sc0: e.scalar_tensor_tensor(
                    out=hout, in0=xin, scalar=stat.v(0, 128, sc0 + 2, [[1, 1]]), in1=gbc.a,
                    op0=ALU.mult, op1=ALU.mult)),
                    reads=[("xt", xs), skey, "gbc"], writes=[("hb", xs)])
                b = nxt("sc", 4)
                pv = Buf(scb3[b].a.bitcast(BF16))
                for kc in range(8):
                    P.add("pe", (lambda e, pv=pv, kc=kc, xs=xs: e.transpose(
                        out=pv.v(0, 128, 128 * kc, [[1, 128]]),
                        in_=hb.v(0, 128, xs * 1024 + 128 * kc, [[1, 128]]), identity=identb.a)),
                        reads=[("hb", xs), "identb"], writes=[sck3[b]])
                eng = "act" if nxt("ev", 2) == 0 else "dve"
                o = hT.v(0, 128, 128 * G, [[512, 8], [1, 128]])
                i = pv.v(0, 128, 0, [[128, 8], [1, 128]])
                if eng == "act":
                    P.add("act", (lambda e, o=o, i=i: e.activation(out=o, in_=i, func=AF.Copy)),
                          reads=[sck3[b]], writes=[("hT", G)])
                else:
                    P.add("dve", (lambda e, o=o, i=i: e.tensor_copy(out=o, in_=i)),
                          reads=[sck3[b]], writes=[("hT", G)])
                yield

        HT_ALL = [("hT", G) for G in range(4)]

        def vblk_nat(B):
            return B % 8

        def vblk_str(m, R):
            return 8 + (m % 5) * 4 + R

        def inproj(n):
            order = [14, 10, 15, 11, 16, 12, 17, 13, 6, 7, 8, 9, 0, 1, 2, 3, 4, 5, 18, 19, 20, 21]
            for gi in order:
                slot = nxt("wr", 3)
                wkey = ("wr", slot)
                P.add("sp", dma1(wring.v(0, 128, slot * 2048, [[1, 2048]]),
                                 DA(wsc_d, gi * 262144, [[2048, 128], [1, 2048]])),
                      reads=["wsc"], writes=[wkey], stream="w%d" % slot)
                if gi == 5:
                    blocks = []
                    for bb in range(4):
                        blocks.append((128 * bb, 1, vblk_nat(4 * n + bb)))
                    for R in range(4):
                        blocks.append((R, 4, vblk_str(n, R)))
                    for poff, pstr, blk in blocks:
                        b = nxt("ip", 3)
                        for kc in range(8):
                            lhsT = hT.v(0, 128, kc * 512 + poff, [[pstr, 128]])
                            rhs = wring.v(0, 128, slot * 2048 + kc * 128, [[1024, 2], [1, 128]])
                            o = ipb[b].v(0, 128, 0, [[1, 256]])
                            P.add("pe", (lambda e, o=o, lhsT=lhsT, rhs=rhs, kc=kc: e.matmul(
                                o, lhsT=lhsT, rhs=rhs, start=(kc == 0), stop=(kc == 7))),
                                reads=HT_ALL + [wkey], writes=[ipk[b]])
                        o = Vb.v(0, 128, blk * 256, [[1, 256]])
                        i = ipb[b].v(0, 128, 0, [[1, 256]])
                        if nxt("ev", 2) == 0:
                            P.add("act", (lambda e, o=o, i=i: e.activation(out=o, in_=i, func=AF.Copy)),
                                  reads=[ipk[b]], writes=[("Vb", blk)])
                        else:
                            P.add("dve", (lambda e, o=o, i=i: e.tensor_copy(out=o, in_=i)),
                                  reads=[ipk[b]], writes=[("Vb", blk)])
                    yield
                    continue
                for ci in range(2):
                    cid = 2 * gi + ci
                    b = nxt("ip", 3)
                    o = ipb[b].a
                    for kc in range(8):
                        lhsT = wring.v(0, 128, slot * 2048 + ci * 1024 + kc * 128, [[1, 128]])
                        rhs = hT.v(0, 128, kc * 512, [[1, 512]])
                        P.add("pe", (lambda e, o=o, lhsT=lhsT, rhs=rhs, kc=kc: e.matmul(
                            o, lhsT=lhsT, rhs=rhs, start=(kc == 0), stop=(kc == 7))),
                            reads=HT_ALL + [wkey], writes=[ipk[b]])
                    i = ipb[b].a
                    if cid < 8:
                        o2 = qT.v(0, 128, cid * 512, [[128, 4], [1, 128]])
                        i2 = ipb[b].v(0, 128, 0, [[1, 4], [4, 128]])
                        P.add("dve", (lambda e, o2=o2, i2=i2: e.tensor_scalar(
                            out=o2, in0=i2, scalar1=0.125, scalar2=None, op0=ALU.mult)),
                            reads=[ipk[b]], writes=[("qT", cid)])
                    elif cid < 10:
                        pr = cid - 8
                        o2 = Kr.v(0, 128, ((n % 5) * 2 + pr) * 512, [[1, 512]])
                        P.add("dve", (lambda e, o2=o2, i=i: e.tensor_copy(out=o2, in_=i)),
                              reads=[ipk[b]], writes=[("Kr", n % 5, pr)])
                    elif 12 <= cid < 20:
                        c = cid - 12
                        o2 = ag.v(0, 128, c * 512, [[1, 512]])
                        P.add("act", (lambda e, o2=o2, i=i: e.activation(out=o2, in_=i, func=AF.Silu)),
                              reads=[ipk[b]], writes=[("ag", c)])
                    elif 20 <= cid < 28:
                        c = cid - 20
                        o2 = hc.v(0, 128, c * 542 + 30, [[1, 512]])
                        s2 = sig.v(0, 128, (c % 2) * 512, [[1, 512]])
                        P.add("dve", (lambda e, o2=o2, i=i, s2=s2: e.tensor_tensor(
                            out=o2, in0=i, in1=s2, op=ALU.mult)),
                            reads=[ipk[b], ("sig", c % 2)], writes=[("hc", c)])
                    elif 28 <= cid < 36:
                        c = cid - 28
                        o2 = sig.v(0, 128, (c % 2) * 512, [[1, 512]])
                        P.add("act", (lambda e, o2=o2, i=i: e.activation(out=o2, in_=i, func=AF.Sigmoid)),
                              reads=[ipk[b]], writes=[("sig", c % 2)])
                    else:
                        c = cid - 36
                        o2 = cg.v(0, 128, c * 512, [[1, 512]])
                        P.add("act", (lambda e, o2=o2, i=i: e.activation(out=o2, in_=i, func=AF.Silu)),
                              reads=[ipk[b]], writes=[("cg", c)])
                yield

        def attention(n):
            units = []
            for g in range(4):
                def s_units(R):
                    return [dict(g=g, kind="s", qidx=R, m=n - dl, koff=R, kstr=4, qoff=R,
                                 vblk=vblk_str(n - dl, R), tab=dl, fin=-1, first=False)
                            for dl in range(min(4, n) + 1)]
                ul = s_units(0)
                ul[0]["first"] = True
                units += ul
                dl1 = []
                for bb in range(4):
                    Bq = 4 * n + bb
                    for be in range(2):
                        Bk = Bq - be
                        if Bk < 0:
                            continue
                        dl1.append(dict(g=g, kind="n", qidx=bb, m=Bk // 4, koff=128 * (Bk % 4), kstr=1,
                                        qoff=128 * bb, vblk=vblk_nat(Bk), tab=7 + be, fin=-1, first=False))
                dl1[-1]["fin"] = 0
                units += dl1
                for R in range(1, 4):
                    ul = s_units(R)
                    ul[-1]["fin"] = R
                    units += ul

            def front(u):
                g = u["g"]
                pr, half = g // 2, g % 2
                P0 = 64 * half
                sb_ = nxt("sc", 4)
                m = u["m"]
                lhsT = Kr.v(P0, 64, ((m % 5) * 2 + pr) * 512 + u["koff"], [[u["kstr"], 128]])
                if u["kind"] == "s":
                    rhs = qT.v(P0, 64, 4 * pr * 512 + 128 * u["qidx"], [[32, 4], [512, 4], [1, 32]])
                else:
                    rhs = qT.v(P0, 64, 4 * pr * 512 + 32 * u["qidx"], [[512, 4], [128, 4], [1, 32]])
                o = scb3[sb_].a
                P.add("pe", (lambda e, o=o, lhsT=lhsT, rhs=rhs: e.matmul(o, lhsT=lhsT, rhs=rhs, start=True, stop=True)),
                      reads=[("Kr", m % 5, pr)] + [("qT", 4 * pr + i) for i in range(4)], writes=[sck3[sb_]])
                pb = nxt("pt", 5)
                u["pb"] = pb
                po = pT.v(0, 128, pb * 512, [[1, 512]])
                P.add("act", (lambda e, po=po, o=o: e.activation(out=po, in_=o, func=AF.Exp)),
                      reads=[sck3[sb_]], writes=[("pT", pb)])
                ev = Et.v(0, 128, (u["tab"] * 4 + g) * 512, [[1, 512]])
                P.add("dve", (lambda e, po=po, ev=ev: e.tensor_tensor(out=po, in0=po, in1=ev, op=ALU.mult)),
                      reads=[("pT", pb), ("Et", u["tab"])], writes=[("pT", pb)])

            started = {}

            def back(u):
                g = u["g"]
                pb = u["pb"]
                vblk = u["vblk"]
                po = pT.v(0, 128, pb * 512, [[1, 512]])
                if u["first"]:
                    for Rb in range(4):
                        started[Rb] = [False, False]
                lv = Vb.v(0, 128, vblk * 256 + g * 64, [[1, 64]])
                lo = onesb.v(0, 128, 0, [[1, 64]])
                if u["kind"] == "s":
                    parts = [(u["qidx"], [[1, 512]], 0, po)]
                else:
                    parts = [(Rp, [[1, 128]], 128 * u["qidx"],
                              pT.v(0, 128, pb * 512 + 32 * Rp, [[128, 4], [1, 32]])) for Rp in range(4)]
                for Rb, odims, ooff, rap in parts:
                    for hv, lt in ((0, lv), (1, lo)):
                        st = not started[Rb][hv]
                        started[Rb][hv] = True
                        oap = accb[Rb].v(64 * hv, 64, ooff, odims)
                        P.add("pe", (lambda e, oap=oap, lt=lt, rap=rap, st=st, hv=hv: e.matmul(
                            oap, lhsT=lt, rhs=rap, start=st, stop=True, skip_group_check=True,
                            tile_position=(0, 64 * hv))),
                            reads=[("Vb", vblk), "onesb", ("pT", pb)], writes=[acck[Rb]])
                if u["fin"] >= 0:
                    R = u["fin"]
                    rs = nxt("rc", 2)
                    ro = rcb.v(0, 64, rs * 512, [[1, 512]])
                    P.add("act", (lambda e, ro=ro, R=R: e.activation(out=ro, in_=accb[R].v(64, 64, 0, [[1, 512]]), func=AF.Ln)),
                          reads=[acck[R]], writes=[("rcb", rs)])
                    P.add("act", (lambda e, ro=ro: e.activation(out=ro, in_=ro, func=AF.Exp, scale=-1.0)),
                          reads=[("rcb", rs)], writes=[("rcb", rs)])
                    for par in range(2):
                        o = yT.v(64 * par, 64, 2 * g * 512 + 128 * R, [[512, 2], [1, 128]])
                        i0 = accb[R].v(0, 64, 32 * par, [[64, 2], [128, 4], [1, 32]])
                        i1 = rcb.v(0, 64, rs * 512 + 32 * par, [[64, 2], [128, 4], [1, 32]])
                        P.add("dve", (lambda e, o=o, i0=i0, i1=i1: e.tensor_tensor(out=o, in0=i0, in1=i1, op=ALU.mult)),
                              reads=[acck[R], ("rcb", rs)], writes=[("yT", 2 * g), ("yT", 2 * g + 1)])
                    o = yT.v(0, 128, 2 * g * 512 + 128 * R, [[512, 2], [1, 128]])
                    a2 = ag.v(0, 128, 2 * g * 512 + R, [[512, 2], [4, 128]])
                    P.add("dve", (lambda e, o=o, a2=a2: e.tensor_tensor(out=o, in0=o, in1=a2, op=ALU.mult)),
                          reads=[("yT", 2 * g), ("yT", 2 * g + 1), ("ag", 2 * g), ("ag", 2 * g + 1)],
                          writes=[("yT", 2 * g), ("yT", 2 * g + 1)])

            AHEAD = 4
            for i in range(min(AHEAD, len(units))):
                front(units[i])
            for i, u in enumerate(units):
                back(u)
                yield
                if i + AHEAD < len(units):
                    front(units[i + AHEAD])

        def conv(n):
            for c in range(8):
                b = 0
                for k0 in range(0, 31, 8):
                    ks = list(range(k0, min(k0 + 8, 31)))
                    slots = []
                    for k in ks:
                        ds = nxt("dg", 16)
                        slots.append(ds)
                        dgo = diag.v(0, 128, ds * 128, [[1, 128]])
                        P.add("pool", (lambda e, dgo=dgo, c=c, k=k: e.tensor_scalar(
                            out=dgo, in0=identb.a, scalar1=cw.v(0, 128, c * 31 + k, [[1, 1]]), scalar2=0.0,
                            op0=ALU.mult, op1=ALU.add)),
                            reads=["identb", "cw"], writes=[("diag", ds)])
                    for k, ds in zip(ks, slots):
                        dgo = diag.v(0, 128, ds * 128, [[1, 128]])
                        rap = hc.v(0, 128, c * 542 + k, [[1, 512]])
                        rd = [("diag", d_) for d_ in slots] if k == ks[0] else [("diag", ds)]
                        P.add("pe", (lambda e, b=b, dgo=dgo, rap=rap, k=k: e.matmul(
                            mmb[b].a, lhsT=dgo, rhs=rap, start=(k == 0), stop=(k == 30))),
                            reads=rd + [("hc", c)], writes=[mmk[b]])
                    yield
                co = cvo.v(0, 128, c * 512, [[1, 512]])
                P.add("act", (lambda e, co=co, b=b, c=c: e.activation(
                    out=co, in_=mmb[b].a, func=AF.Identity, bias=vec.v(0, 128, c, [[1, 1]]))),
                    reads=[mmk[b], "vec"], writes=[("cvo", c)])
            P.add("pool", lambda e: e.tensor_copy(out=hc.v(0, 128, 0, [[542, 8], [1, 30]]),
                                                  in_=hc.v(0, 128, 512, [[542, 8], [1, 30]])),
                  reads=[("hc", c) for c in range(8)], writes=[("hc", c) for c in range(8)])
            yield

        def lnorm(n):
            for c in range(8):
                co = cvo.v(0, 128, c * 512, [[1, 512]])
                xs_ = c % 2
                qo = xsq.v(0, 128, xs_ * 512, [[1, 512]])
                if c % 2 == 0:
                    P.add("act", (lambda e, qo=qo, co=co: e.activation(out=qo, in_=co, func=AF.Square)),
                          reads=[("cvo", c)], writes=[("xsq", xs_)])
                else:
                    P.add("dve", (lambda e, qo=qo, co=co: e.tensor_tensor(out=qo, in0=co, in1=co, op=ALU.mult)),
                          reads=[("cvo", c)], writes=[("xsq", xs_)])
                for (bank, key, rap, rkey) in ((scb[0], sck[0], co, ("cvo", c)), (scb[1], sck[1], qo, ("xsq", xs_))):
                    P.add("pe", (lambda e, bank=bank, rap=rap, c=c: e.matmul(
                        bank.a, lhsT=onesb.a, rhs=rap, start=(c == 0), stop=(c == 7))),
                        reads=["onesb", rkey], writes=[key])
                yield
            mean = lnw.v(0, 128, 0, [[1, 512]])
            msq = lnw.v(0, 128, 512, [[1, 512]])
            rstd = lnw.v(0, 128, 1024, [[1, 512]])
            P.add("act", lambda e: e.activation(out=mean, in_=scb[0].a, func=AF.Copy, scale=1.0 / 1024.0),
                  reads=[sck[0]], writes=[("ln", 0)])
            P.add("act", lambda e: e.activation(out=msq, in_=scb[0].a, func=AF.Square, scale=1.0 / 1024.0),
                  reads=[sck[0]], writes=[("ln", 1)])
            P.add("dve", lambda e: e.scalar_tensor_tensor(out=msq, in0=scb[1].a, scalar=1.0 / 1024.0, in1=msq,
                                                          op0=ALU.mult, op1=ALU.subtract),
                  reads=[sck[1], ("ln", 1)], writes=[("ln", 1)])
            P.add("act", lambda e: e.activation(out=msq, in_=msq, func=AF.Sqrt, bias=epsc.v(0, 128, 1, [[1, 1]])),
                  reads=[("ln", 1), "eps"], writes=[("ln", 1)])
            P.add("dve", lambda e: e.reciprocal(out=rstd, in_=msq), reads=[("ln", 1)], writes=[("ln", 2)])
            for c in range(8):
                co = cvo.v(0, 128, c * 512, [[1, 512]])
                t = lnw.v(0, 128, 1536, [[1, 512]])
                P.add("dve", (lambda e, co=co, t=t: e.tensor_tensor(out=t, in0=co, in1=mean, op=ALU.subtract)),
                      reads=[("cvo", c), ("ln", 0)], writes=[("ln", 3)])
                P.add("dve", (lambda e, t=t: e.tensor_tensor(out=t, in0=t, in1=rstd, op=ALU.mult)),
                      reads=[("ln", 3), ("ln", 2)], writes=[("ln", 3)])
                z = xb16.v(0, 128, (c % 2) * 512, [[1, 512]])
                P.add("act", (lambda e, z=z, t=t, c=c: e.activation(
                    out=z, in_=t, func=AF.Silu, scale=vec.v(0, 128, 8 + c, [[1, 1]]), bias=vec.v(0, 128, 16 + c, [[1, 1]]))),
                    reads=[("ln", 3), "vec"], writes=[("xb16", c % 2)])
                yo = yT.v(0, 128, (8 + c) * 512, [[1, 512]])
                gi_ = cg.v(0, 128, c * 512, [[1, 512]])
                P.add("dve", (lambda e, yo=yo, z=z, gi_=gi_: e.tensor_tensor(out=yo, in0=z, in1=gi_, op=ALU.mult)),
                      reads=[("xb16", c % 2), ("cg", c)], writes=[("yT", 8 + c)])
                yield

        wo_pref = {}

        def wo_dma(hf, kc):
            ws = nxt("wo", 4)
            P.add("pool", dma1(woring.v(0, 128, ws * 512, [[1, 512]]),
                               DA(wosc_d, hf * 1048576 + kc * 65536, [[512, 128], [1, 512]])),
                  reads=["wosc"], writes=[("wo", ws)], stream="wo%d" % ws)
            return ws

        def outproj_prefetch(n):
            wo_pref[n] = [wo_dma(0, kc) for kc in range(4)]

        def outproj(n):
            for R in range(4):
                P.add("sp", dma1(res.v(0, 128, R * 1024, [[1, 1024]]),
                                 DA(x_d, (512 * n + R) * 1024, [[4 * 1024, 128], [1, 1024]])),
                      writes=[("res", R)], stream="r%d" % R)
            for hf in range(2):
                banks = (accb, acck) if hf == 0 else (scb3, sck3)
                for kc in range(16):
                    if hf == 0 and kc < 4 and wo_pref.get(n):
                        ws = wo_pref[n][kc]
                    else:
                        ws = wo_dma(hf, kc)
                    for R in range(4):
                        if kc < 8:
                            lhsT = yT.v(0, 128, kc * 512 + 128 * R, [[1, 128]])
                        else:
                            lhsT = yT.v(0, 128, kc * 512 + R, [[4, 128]])
                        rhs = woring.v(0, 128, ws * 512, [[1, 512]])
                        P.add("pe", (lambda e, R=R, lhsT=lhsT, rhs=rhs, kc=kc, bk=banks[0]: e.matmul(
                            bk[R].a, lhsT=lhsT, rhs=rhs, start=(kc == 0), stop=(kc == 15))),
                            reads=[("yT", kc), ("wo", ws)], writes=[banks[1][R]])
                for R in range(4):
                    r = res.v(0, 128, R * 1024 + 512 * hf, [[1, 512]])
                    P.add("dve", (lambda e, r=r, R=R, bk=banks[0]: e.tensor_tensor(out=r, in0=bk[R].a, in1=r, op=ALU.add)),
                          reads=[banks[1][R], ("res", R)], writes=[("res", R)])

        def outproj_tail(n):
            for R in range(4):
                r = res.v(0, 128, R * 1024, [[1, 1024]])
                sc0 = nxt("st", 8) * 4
                skey = ("stat", sc0)
                jo = xsq.v(0, 128, 0, [[1, 1024]])
                P.add("act", (lambda e, r=r, sc0=sc0, jo=jo: e.activation(
                    out=jo, in_=r, func=AF.Square, accum_out=stat.v(0, 128, sc0, [[1, 1]]))),
                    reads=[("res", R)], writes=[skey, ("xsq", 0), ("xsq", 1)])
                P.add("act", (lambda e, sc0=sc0: e.activation(
                    out=stat.v(0, 128, sc0 + 1, [[1, 1]]), in_=stat.v(0, 128, sc0, [[1, 1]]), func=AF.Ln,
                    scale=1.0 / 1024.0, bias=epsc.v(0, 128, 0, [[1, 1]]))),
                    reads=[skey, "eps"], writes=[skey])
                P.add("act", (lambda e, sc0=sc0: e.activation(
                    out=stat.v(0, 128, sc0 + 2, [[1, 1]]), in_=stat.v(0, 128, sc0 + 1, [[1, 1]]), func=AF.Exp,
                    scale=-0.5)),
                    reads=[skey], writes=[skey])
                P.add("dve", (lambda e, r=r, sc0=sc0: e.scalar_tensor_tensor(
                    out=r, in0=r, scalar=stat.v(0, 128, sc0 + 2, [[1, 1]]), in1=fgbc.a,
                    op0=ALU.mult, op1=ALU.mult)),
                    reads=[("res", R), skey, "fgbc"], writes=[("res", R)])
                P.add("pool", dma1(DA(out_d, (512 * n + R) * 1024, [[4 * 1024, 128], [1, 1024]]), r),
                      reads=[("res", R)], writes=["out"], stream="s%d" % R)
                yield

        def interleave(*gw):
            gens = [[g_, w_] for g_, w_ in gw]
            while gens:
                for it in list(gens):
                    for _ in range(it[1]):
                        try:
                            next(it[0])
                        except StopIteration:
                            gens.remove(it)
                            break

        def adv(gen, k):
            if gen is None:
                return None
            for _ in range(k):
                try:
                    next(gen)
                except StopIteration:
                    return None
            return gen

        def inproj_phase(n, ln_tile):
            gi, gc = inproj(n), conv(n)
            gl = lnorm(ln_tile) if ln_tile is not None else None
            step = 0
            while gi is not None or gc is not None or gl is not None:
                gl = adv(gl, 1)
                gi = adv(gi, 1)
                if step >= 8:
                    gc = adv(gc, 3)
                step += 1

        interleave((pre(0), 1))
        inproj_phase(0, None)
        for n in range(NT):
            tails = [(outproj_tail(n - 1), 1)] if n > 0 else []
            if n + 1 < NT:
                interleave((attention(n), 4), (pre(n + 1), 1), *tails)
                outproj_prefetch(n)
                inproj_phase(n + 1, n)
            else:
                interleave((attention(n), 4), *tails)
                outproj_prefetch(n)
                interleave((lnorm(n), 1))
            outproj(n)
        interleave((outproj_tail(NT - 1), 1))

        with nc.Block() as block:
            P.emit(nc, block, esem, ssem, {"pool": ["s0", "s1", "s2", "s3"]})
    return nc


_CACHE = {}


def _common_inputs(norm_g, w_in, conv_w, conv_b, conv_ln_g, conv_ln_b, w_out, final_norm_g):
    f = np.float32
    cw = np.ascontiguousarray(
        np.asarray(conv_w, f)[0].reshape(31, 8, 128).transpose(2, 1, 0).reshape(128, 248))
    vecs = [np.asarray(v, f)[0].reshape(8, 128).T for v in (conv_b, conv_ln_g, conv_ln_b)]
    vec = np.ascontiguousarray(np.concatenate(vecs, axis=1))
    return {
        "w_in": np.ascontiguousarray(np.asarray(w_in, f)[0]),
        "w_out": np.ascontiguousarray(np.asarray(w_out, f)[0]),
        "g": np.ascontiguousarray(np.asarray(norm_g, f).reshape(1, 1024)),
        "fg": np.ascontiguousarray(np.asarray(final_norm_g, f).reshape(1, 1024)),
        "cw": cw,
        "vec": vec,
        "dtab": build_dtab(),
        "ident": np.eye(128, dtype=f),
    }


def kernel(x, norm_g, w_in, conv_w, conv_b, conv_ln_g, conv_ln_b, w_out, final_norm_g):
    x = np.asarray(x, np.float32)
    B, S, _ = x.shape
    NT = S // 512
    if NT not in _CACHE:
        _CACHE[NT] = build(NT)
    nc = _CACHE[NT]
    common = _common_inputs(norm_g, w_in, conv_w, conv_b, conv_ln_g, conv_ln_b, w_out, final_norm_g)
    in_maps = []
    for b in range(B):
        m = dict(common)
        m["x"] = np.ascontiguousarray(x[b])
        in_maps.append(m)
    res = run_bass_kernel_spmd(nc, in_maps, core_ids=list(range(B)))
    return np.stack([np.asarray(r["out"], np.float32) for r in res.results], axis=0)
```
